# Optimizing a Trainium2 kernel written in Bass

```python
import jax, jax.numpy as jnp
from jax import lax
import numpy as np

D_MODEL = 1024
BATCH = 2
SEQ = 8192
DEPTH = 1

GRID_W = 64
N_HEADS = 8
HEAD_DIM = 64
ATTN_WIDTH = N_HEADS * HEAD_DIM
CONV_WIDTH = D_MODEL // 2
CONV_K = 3
WIN_ROWS_MAX = 8
WIN_COLS = 16
Q_COL_BLOCK = 16
KEY_COL_SPAN = 32
D_FF = 4 * D_MODEL
EPS = 1e-6
NEG_INF = -1e30
PROJ_SPLITS = [ATTN_WIDTH, ATTN_WIDTH, ATTN_WIDTH, CONV_WIDTH, CONV_WIDTH, CONV_WIDTH, D_MODEL]
PROJ_WIDTH = 3 * ATTN_WIDTH + 3 * CONV_WIDTH + 2 * D_MODEL

kernel_name = "hybrid_natten2d_shortconv_gated_encoder"


def rms_norm(x, g):
    xf = x.astype(jnp.float32)
    xf = xf * lax.rsqrt(jnp.mean(xf * xf, axis=-1, keepdims=True) + EPS)
    return xf.astype(x.dtype) * g


def _na_indices(rows):
    kr = min(WIN_ROWS_MAX, rows)
    r = np.arange(rows)
    row_start = np.clip(r - kr // 2, 0, rows - kr)
    key_rows = row_start[:, None] + np.arange(kr)[None, :]
    rel_r = key_rows - r[:, None] + WIN_ROWS_MAX - 1
    ncb = GRID_W // Q_COL_BLOCK
    j = np.arange(ncb)
    cb_start = np.clip(j * Q_COL_BLOCK - WIN_COLS // 2, 0, GRID_W - KEY_COL_SPAN)
    key_cols = cb_start[:, None] + np.arange(KEY_COL_SPAN)[None, :]
    q_cols = j[:, None] * Q_COL_BLOCK + np.arange(Q_COL_BLOCK)[None, :]
    win_start = np.clip(q_cols - WIN_COLS // 2, 0, GRID_W - WIN_COLS)
    kc = key_cols[:, None, :]
    col_mask = (kc >= win_start[..., None]) & (kc < win_start[..., None] + WIN_COLS)
    rel_c = np.clip(kc - q_cols[..., None] + WIN_COLS - 1, 0, 2 * WIN_COLS - 2)
    key_tok = key_rows[:, None, :, None] * GRID_W + key_cols[None, :, None, :]
    key_tok = key_tok.reshape(rows, ncb, kr * KEY_COL_SPAN).astype(np.int32)
    mask = np.broadcast_to(col_mask[:, :, None, :], (ncb, Q_COL_BLOCK, kr, KEY_COL_SPAN))
    mask = mask.reshape(ncb, Q_COL_BLOCK, kr * KEY_COL_SPAN)
    return kr, key_tok, rel_r.astype(np.int32), rel_c.astype(np.int32), mask


def neighbourhood_attention(q, k, v, rpb):
    B, S, H, dh = q.shape
    rows = S // GRID_W
    kr, key_tok, rel_r, rel_c, mask = _na_indices(rows)
    ncb = GRID_W // Q_COL_BLOCK
    qb = q.reshape(B, rows, ncb, Q_COL_BLOCK, H, dh)
    kb = k[:, key_tok]
    vb = v[:, key_tok]
    s = jnp.einsum('brjqhd,brjkhd->bhrjqk', qb, kb).astype(jnp.float32) * (dh ** -0.5)
    bias = rpb[:, rel_r[:, None, None, :, None], rel_c[None, :, :, None, :]]
    bias = bias.reshape(H, rows, ncb, Q_COL_BLOCK, kr * KEY_COL_SPAN).astype(jnp.float32)
    s = jnp.where(jnp.asarray(mask), s + bias[None], NEG_INF)
    p = jax.nn.softmax(s, axis=-1).astype(v.dtype)
    o = jnp.einsum('bhrjqk,brjkhd->brjqhd', p, vb)
    return o.reshape(B, S, H * dh)


def short_gated_conv(cb, cc, ch, w):
    z = cc * ch
    zp = jnp.pad(z, ((0, 0), (1, 1), (0, 0)))
    conv = zp[:, :-2] * w[0] + zp[:, 1:-1] * w[1] + zp[:, 2:] * w[2]
    return cb * conv


def setup_inputs(seed: int = 0) -> dict:
    key = jax.random.key(seed)
    ks = jax.random.split(key, 16)
    f32 = jnp.float32

    def nrm(k, shape, scale):
        return jax.random.normal(k, shape, f32) * scale

    return {
        "x": nrm(ks[0], (BATCH, SEQ, D_MODEL), 1.0),
        "norm1_g": 1.0 + nrm(ks[1], (DEPTH, D_MODEL), 0.02),
        "w_in": nrm(ks[2], (DEPTH, D_MODEL, PROJ_WIDTH), D_MODEL ** -0.5),
        "q_norm_g": 1.0 + nrm(ks[3], (DEPTH, HEAD_DIM), 0.02),
        "k_norm_g": 1.0 + nrm(ks[4], (DEPTH, HEAD_DIM), 0.02),
        "rpb": nrm(ks[5], (DEPTH, N_HEADS, 2 * WIN_ROWS_MAX - 1, 2 * WIN_COLS - 1), 0.5),
        "conv_w": nrm(ks[6], (DEPTH, CONV_K, CONV_WIDTH), CONV_K ** -0.5),
        "w_attn_branch": nrm(ks[7], (DEPTH, ATTN_WIDTH, D_MODEL), ATTN_WIDTH ** -0.5),
        "w_conv_branch": nrm(ks[8], (DEPTH, CONV_WIDTH, D_MODEL), CONV_WIDTH ** -0.5),
        "w_o": nrm(ks[9], (DEPTH, D_MODEL, D_MODEL), D_MODEL ** -0.5),
        "norm2_g": 1.0 + nrm(ks[10], (DEPTH, D_MODEL), 0.02),
        "w_mlp_in": nrm(ks[11], (DEPTH, D_MODEL, D_FF), D_MODEL ** -0.5),
        "w_mlp_out": nrm(ks[12], (DEPTH, D_FF, D_MODEL), D_FF ** -0.5),
    }


def reference(x, norm1_g, w_in, q_norm_g, k_norm_g, rpb, conv_w, w_attn_branch,
              w_conv_branch, w_o, norm2_g, w_mlp_in, w_mlp_out):
    B, S, _ = x.shape
    split_pts = [int(p) for p in np.cumsum(PROJ_SPLITS)]
    for l in range(DEPTH):
        u = rms_norm(x, norm1_g[l])
        proj = u @ w_in[l]
        q, k, v, cb, cc, ch, ga, gb = jnp.split(proj, split_pts, axis=-1)
        q = rms_norm(q.reshape(B, S, N_HEADS, HEAD_DIM), q_norm_g[l])
        k = rms_norm(k.reshape(B, S, N_HEADS, HEAD_DIM), k_norm_g[l])
        v = v.reshape(B, S, N_HEADS, HEAD_DIM)
        y_a = neighbourhood_attention(q, k, v, rpb[l]) @ w_attn_branch[l]
        y_b = short_gated_conv(cb, cc, ch, conv_w[l]) @ w_conv_branch[l]
        merged = jax.nn.sigmoid(ga) * y_a + jax.nn.sigmoid(gb) * y_b
        x = x + merged @ w_o[l]
        h = rms_norm(x, norm2_g[l]) @ w_mlp_in[l]
        x = x + jnp.square(jax.nn.relu(h)) @ w_mlp_out[l]
    return x
```

```python
import contextlib
import numpy as np
import concourse.bass as bass
import concourse.mybir as mybir
from concourse.bass_utils import run_bass_kernel_spmd

F32 = mybir.dt.float32
BF16 = mybir.dt.bfloat16
ALU = mybir.AluOpType
AF = mybir.ActivationFunctionType

N_CORES = 8
D = 1024
PROJ = 5120
DFF = 4096
TOK = 2048
TEXT = 2560
OWN0 = 256
NEG = -30000.0
EPS = 1e-6
PAIR_ORDER = [0, 2, 3, 4, 5, 1, 6, 7, 8, 9, 14, 10, 11, 12, 13, 15]
SPECIAL = {0: 1, 1: 2, 14: 3, 15: 4}


class T:
    def __init__(self):
        self.w = None
        self.r = {}


class Prog:
    def __init__(self, nc, es):
        self.nc = nc
        self.es = es
        self.eng = {"pe": nc.tensor, "act": nc.scalar, "dve": nc.vector,
                    "pool": nc.gpsimd, "sp": nc.sync}
        self.sem = {}
        self.cnt = {}
        self.seen = {}
        self.chans = []
        for n in ("pe", "act", "dve"):
            self.sem[n] = es.enter_context(nc.semaphore("s_" + n))
            self.cnt[n] = 0

    def chan(self, name):
        self.sem[name] = self.es.enter_context(self.nc.semaphore("c_" + name))
        self.cnt[name] = 0
        self.chans.append(name)
        return name

    def wait(self, e, *toks):
        for t in toks:
            if t is None:
                continue
            n, v = t
            if self.seen.get((e, n), 0) < v:
                self.eng[e].wait_ge(self.sem[n], v)
                self.seen[(e, n)] = v

    def _deps(self, reads, writes):
        toks = []
        for b in reads:
            toks.append(b.w)
        for b in writes:
            toks.append(b.w)
            toks.extend(b.r.items())
        return toks

    def _commit(self, tok, reads, writes):
        for b in reads:
            if b.r.get(tok[0], 0) < tok[1]:
                b.r[tok[0]] = tok[1]
        for b in writes:
            b.w = tok
            b.r = {}

    def op(self, e, make, reads=(), writes=(), extra=()):
        self.wait(e, *self._deps(reads, writes), *extra)
        inst = make()
        if isinstance(inst, (list, tuple)):
            inst = inst[-1]
        inst.then_inc(self.sem[e], 1)
        self.cnt[e] += 1
        tok = (e, self.cnt[e])
        self._commit(tok, reads, writes)
        return tok

    def dma(self, q, ch, out, in_, reads=(), writes=(), extra=()):
        self.wait(q, *self._deps(reads, writes), *extra)
        inst = self.eng[q].dma_start(out=out, in_=in_)
        inst.then_inc(self.sem[ch], 16)
        self.cnt[ch] += 16
        tok = (ch, self.cnt[ch])
        self._commit(tok, reads, writes)
        return tok

    def dma_multi(self, q, ch, pairs, reads=(), writes=()):
        self.wait(q, *self._deps(reads, writes))
        for out, in_ in pairs:
            inst = self.eng[q].dma_start(out=out, in_=in_)
            inst.then_inc(self.sem[ch], 16)
            self.cnt[ch] += 16
        tok = (ch, self.cnt[ch])
        self._commit(tok, reads, writes)
        return tok

    def barrier(self):
        toks = [(n, self.cnt[n]) for n in ("pe", "act", "dve") + tuple(self.chans)
                if self.cnt[n] > 0]
        for e in self.eng:
            self.wait(e, *toks)


class Stream:
    def __init__(self, P, q, name, nslots, slot_aps):
        self.P = P
        self.q = q
        self.n = nslots
        self.aps = slot_aps
        self.tb = [T() for _ in range(nslots)]
        self.ch = [P.chan(f"{name}{i}") for i in range(nslots)]
        self.srcs = []
        self.issued = 0

    def add(self, src):
        self.srcs.append(src)
        return len(self.srcs) - 1

    def ensure(self, i, extra=()):
        self.issue_upto(i + self.n - 1, extra)

    def issue_upto(self, last, extra=()):
        while self.issued <= min(last, len(self.srcs) - 1):
            k = self.issued
            s = k % self.n
            self.P.dma(self.q, self.ch[s], self.aps[s], self.srcs[k], writes=[self.tb[s]], extra=extra)
            self.issued += 1

    def slot(self, i):
        return i % self.n


def build_nc(stop_after=None, dump=()):
    nc = bass.Bass("TRN2", target_bir_lowering=False)
    x_ext = nc.dram_tensor("x_ext", [TEXT, D], F32, kind="ExternalInput").ap()
    btab = nc.dram_tensor("btab", [5, 128, 5120], F32, kind="ExternalInput").ap()
    smallp = nc.dram_tensor("smallp", [128, 32], F32, kind="ExternalInput").ap()
    consts = nc.dram_tensor("consts", [128, 256], F32, kind="ExternalInput").ap()
    w_in = nc.dram_tensor("w_in", [D, PROJ], F32, kind="ExternalInput").ap()
    w_a = nc.dram_tensor("w_a", [512, D], F32, kind="ExternalInput").ap()
    w_b = nc.dram_tensor("w_b", [512, D], F32, kind="ExternalInput").ap()
    w_o = nc.dram_tensor("w_o", [D, D], F32, kind="ExternalInput").ap()
    w1 = nc.dram_tensor("w1", [D, DFF], F32, kind="ExternalInput").ap()
    w2 = nc.dram_tensor("w2", [DFF, D], F32, kind="ExternalInput").ap()
    out = nc.dram_tensor("out", [TOK, D], F32, kind="ExternalOutput").ap()
    dumps = {}
    for name, shape, dt in dump:
        dumps[name] = nc.dram_tensor("dbg_" + name, list(shape), dt, kind="ExternalOutput").ap()

    with contextlib.ExitStack() as es:
        P = Prog(nc, es)
        sb = lambda name, shape, dt: es.enter_context(nc.sbuf_tensor(name, shape, dt))
        U1 = sb("u1", [128, 40960], BF16)
        U2 = sb("u2", [128, 36864], BF16)
        WS = sb("ws", [128, 3, 4096], BF16)
        TM = sb("tm", [128, 14848], BF16)
        smp = sb("smp", [128, 32], F32)
        gs = sb("gs", [128, 2], F32)
        cf = sb("cf", [128, 256], F32)
        cb16 = sb("cb16", [128, 256], BF16)
        stat = sb("stat", [128, 64], F32)
        ps = es.enter_context(nc.psum_tensor("ps", [128, 4096], F32))
        ident = cb16[:, 0:128]
        bones = cb16[:, 128:256]

        def bank(b, n=512):
            return ps[:, b * 512:b * 512 + n]

        def bank16(b):
            return ps[:, b * 512:(b + 1) * 512].bitcast(BF16)

        bankT = [T() for _ in range(8)]

        def u1f32(off_b, n):
            return U1[:, off_b // 2: off_b // 2 + 2 * n].bitcast(F32)

        def tmf32(off_b, n):
            return TM[:, off_b // 2: off_b // 2 + 2 * n].bitcast(F32)

        def tm16(off_b, n):
            return TM[:, off_b // 2: off_b // 2 + n]

        xnT = U2[:, 0:20480].rearrange("p (c n) -> p c n", c=8)
        AT = U2[:, 20480:28672].rearrange("p (c n) -> p c n", c=4)
        BT = U2[:, 28672:36864].rearrange("p (c n) -> p c n", c=4)
        V = U1[:, 0:10400].rearrange("p (t h d) -> p t h d", t=20, h=8)
        Egen = U1[:, 10400:15520].rearrange("p (h n) -> p h n", h=8)
        Espc = U1[:, 15520:20640].rearrange("p (h n) -> p h n", h=8)
        kT = U1[:, 20640:30880].rearrange("p (c n) -> p c n", c=4)
        qT = U1[:, 32768:40960].rearrange("p (c n) -> p c n", c=4)
        ws3 = [WS[:, s, :].rearrange("p (c n) -> p c n", c=8) for s in range(3)]

        c_small = P.chan("small")
        t_small = T()
        P.dma_multi("sp", c_small, [(smp[:], smallp[:, :]), (cf[:], consts[:, :])], writes=[t_small])
        t_c16 = T()
        P.op("dve", lambda: nc.vector.tensor_copy(out=cb16[:], in_=cf[:]), reads=[t_small], writes=[t_c16])
        t_gs = T()
        P.op("dve", lambda: [
            nc.vector.tensor_scalar(out=gs[:, 0:1], in0=smp[:, 16:17], scalar1=0.125, scalar2=None, op0=ALU.mult),
            nc.vector.tensor_copy(out=gs[:, 1:2], in_=smp[:, 17:18]),
            nc.vector.memset(stat[:], 0.0)], reads=[t_small], writes=[t_gs])

        win = Stream(P, "pool", "win", 3, ws3)
        for s in (1, 0, 2, 4, 5, 3):
            win.add(w_in[:, s * 512:(s + 1) * 512].rearrange("(c p) n -> p c n", p=128))

        NXS = 8
        xin = [u1f32(k * 4096, 1024) for k in range(NXS)]
        xin_t = [T() for _ in range(NXS)]
        xin_ch = [P.chan(f"xin{k}") for k in range(NXS)]
        xnT_t = [[T() for _ in range(8)] for _ in range(5)]
        xst = [U1[:, 16384:20480].rearrange("p (i d) -> p i d", i=4),
               tm16(17408, 4096).rearrange("p (i d) -> p i d", i=4)]
        xst_t = [[T() for _ in range(4)] for _ in range(2)]
        junk = tm16(0, 1024)
        t_junk = T()
        ms_t = [T() for _ in range(5)]
        rs_t = [T() for _ in range(5)]

        def load_x(t):
            P.dma("sp", xin_ch[t % NXS], xin[t % NXS], x_ext[t * 128:(t + 1) * 128, :], writes=[xin_t[t % NXS]])

        for t in range(NXS):
            load_x(t)
        evc = [0]

        def stage1a(g, i):
            t = 4 * g + i
            P.op("act", lambda: nc.scalar.activation(
                out=junk, in_=xin[t % NXS], func=AF.Square, scale=1.0 / 32.0,
                accum_out=stat[:, t:t + 1]),
                reads=[xin_t[t % NXS], t_gs], writes=[ms_t[g], t_junk] if i == 3 else [t_junk])

        def stage1b(g):
            P.op("act", lambda: nc.scalar.activation(
                out=stat[:, 20 + 4 * g:24 + 4 * g], in_=stat[:, 4 * g:4 * g + 4], func=AF.Ln, bias=EPS),
                reads=[ms_t[g]], writes=[rs_t[g]])
            P.op("act", lambda: nc.scalar.activation(
                out=stat[:, 40 + 4 * g:44 + 4 * g], in_=stat[:, 20 + 4 * g:24 + 4 * g], func=AF.Exp, scale=-0.5),
                reads=[rs_t[g]], writes=[rs_t[g]])
            for i in range(4):
                t = 4 * g + i
                P.op("dve", lambda: nc.vector.tensor_scalar(
                    out=xst[g % 2][:, i, :], in0=xin[t % NXS], scalar1=stat[:, 40 + t:41 + t],
                    scalar2=None, op0=ALU.mult),
                    reads=[xin_t[t % NXS], rs_t[g]], writes=[xst_t[g % 2][i]])
                if t + NXS < 20:
                    load_x(t + NXS)

        def stage1(g):
            for i in range(4):
                stage1a(g, i)
            stage1b(g)

        def stage2(g):
            for c in range(8):
                b_ = evc[0] % 4
                evc[0] += 1
                P.op("pe", lambda: [
                    nc.tensor.transpose(bank16(b_)[:, i * 128:(i + 1) * 128],
                                        xst[g % 2][:, i, c * 128:(c + 1) * 128], ident)
                    for i in range(4)],
                    reads=xst_t[g % 2] + [t_c16], writes=[bankT[b_]])
                dst = xnT[:, c, g * 512:(g + 1) * 512]
                if c % 4 != 3:
                    P.op("dve", lambda: nc.vector.tensor_scalar(
                        out=dst, in0=bank16(b_)[:, 0:512], scalar1=smp[:, c:c + 1], scalar2=None,
                        op0=ALU.mult), reads=[bankT[b_]], writes=[xnT_t[g][c]])
                else:
                    P.op("act", lambda: nc.scalar.activation(
                        out=dst, in_=bank16(b_)[:, 0:512], func=AF.Copy, scale=smp[:, c:c + 1]),
                        reads=[bankT[b_]], writes=[xnT_t[g][c]])

        NEST = 4
        est = [tmf32(2048, 640), tmf32(4608, 640), tmf32(12288, 640), tmf32(14848, 640)]
        est_t = [T() for _ in range(NEST)]
        est_ch = [P.chan(f"est{k}") for k in range(NEST)]
        E_t = {"gen": T(), "spc": T()}
        esteps = []
        edma = [0]
        edone = [0]

        def E_dma_upto(last):
            while edma[0] <= min(last, len(esteps) - 1):
                kk = edma[0]
                v_, h_, _, _ = esteps[kk]
                P.dma("sp", est_ch[kk % NEST], est[kk % NEST], btab[v_, :, h_ * 640:(h_ + 1) * 640],
                      writes=[est_t[kk % NEST]])
                edma[0] += 1

        def queue_E(variant, dst, key, prefetch=True):
            for h in range(8):
                esteps.append((variant, h, dst, key))
            if prefetch:
                E_dma_upto(edone[0] + NEST - 1)

        def E_tick(nsteps=1):
            for _ in range(nsteps):
                k = edone[0]
                if k >= len(esteps):
                    return
                E_dma_upto(k + NEST - 1)
                v_, h_, dst, key = esteps[k]
                P.op("act", lambda: nc.scalar.activation(out=dst[:, h_, :], in_=est[k % NEST], func=AF.Exp),
                     reads=[est_t[k % NEST]], writes=[E_t[key]])
                edone[0] += 1

        def E_flush():
            E_tick(len(esteps))

        queue_E(0, Egen, "gen", prefetch=False)
        queue_E(1, Espc, "spc", prefetch=False)

        sq = [tm16(7168 + k * 1024, 512) for k in range(2)]
        lnt = [tmf32(9216 + k * 2048, 512) for k in range(2)]
        rsb = [tmf32(13312 + k * 2048, 512) for k in range(2)]
        sq_t, ln_t, rb_t = [T(), T()], [T(), T()], [T(), T()]

        def qk_items(seq):
            def proj(n, it):
                slot, p, tok0, deps, dst, gcol = it
                X = (2 * n) % 8
                P.op("pe", lambda: [nc.tensor.matmul(
                    bank(X), lhsT=ws3[slot][:, c, p * 128:(p + 1) * 128],
                    rhs=xnT[:, c, tok0: tok0 + 512],
                    start=(c == 0), stop=(c == 7)) for c in range(8)],
                    reads=[win.tb[slot]] + deps, writes=[bankT[X]])
                P.op("act", lambda: nc.scalar.activation(out=sq[n % 2], in_=bank(X), func=AF.Square),
                     reads=[bankT[X]], writes=[sq_t[n % 2]])

            def rest(n, it):
                slot, p, tok0, deps, dst, gcol = it
                X, Y = (2 * n) % 8, (2 * n + 1) % 8
                P.op("pe", lambda: nc.tensor.matmul(bank(Y), lhsT=bones, rhs=sq[n % 2], start=True, stop=True),
                     reads=[sq_t[n % 2]], writes=[bankT[Y]])
                P.op("act", lambda: nc.scalar.activation(out=lnt[n % 2], in_=bank(Y), func=AF.Ln,
                                                         scale=1.0 / 64.0, bias=EPS),
                     reads=[bankT[Y]], writes=[ln_t[n % 2]])
                P.op("act", lambda: nc.scalar.activation(out=rsb[n % 2], in_=lnt[n % 2], func=AF.Exp, scale=-0.5),
                     reads=[ln_t[n % 2]], writes=[rb_t[n % 2]])
                P.op("dve", lambda: nc.vector.scalar_tensor_tensor(
                    out=dst, in0=bank(X), scalar=gs[:, gcol:gcol + 1],
                    in1=rsb[n % 2], op0=ALU.mult, op1=ALU.mult),
                    reads=[bankT[X], rb_t[n % 2]])

            prev = None
            n = 0
            for e in seq:
                if callable(e):
                    e()
                    continue
                proj(n, e)
                if prev is not None:
                    rest(*prev)
                prev = (n, e)
                n += 1
            rest(*prev)

        sK, sQ = win.slot(0), win.slot(1)

        def K_items(eg):
            return [(sK, p, eg * 512, list(xnT_t[eg]), kT[:, p, eg * 512:(eg + 1) * 512], 1) for p in range(4)]

        def Q_items(tg):
            return [(sQ, p, OWN0 + tg * 512, xnT_t[tg] + xnT_t[tg + 1], qT[:, p, tg * 512:(tg + 1) * 512], 0)
                    for p in range(4)]

        win.issue_upto(0, extra=[xin_t[3].w])
        win.issue_upto(1, extra=[xin_t[7].w])
        stage1(0)
        stage1(1)
        stage2(0)
        def weave(items, g):
            out = []
            for i, it in enumerate(items):
                out.append(it)
                out.append(lambda g=g, i=i: stage1a(g, i))
            return out

        seq = weave(K_items(0), 2)
        seq += [lambda: (stage1b(2), win.issue_upto(2, extra=[xin_t[19 % NXS].w]), stage2(1))]
        seq += Q_items(0)
        seq += weave(K_items(1), 3)
        seq += [lambda: (stage1b(3), stage2(2))]
        seq += weave(Q_items(1), 4)
        seq += [lambda: stage1b(4)]
        seq += K_items(2)
        seq += [lambda: stage2(3)]
        seq += Q_items(2) + K_items(3)
        seq += [lambda: stage2(4)]
        seq += Q_items(3) + K_items(4)
        qk_items(seq)
        P.barrier()
        P.op("dve", lambda: nc.vector.memset(V[:, :, :, 64:65], 1.0))

        for k_ in (2, 3):
            est_t[k_].r = {"act": P.cnt["act"], "dve": P.cnt["dve"], "pe": P.cnt["pe"]}
        win.ensure(2)
        slot = win.slot(2)
        for t in range(20):
            b = t % 8
            P.op("pe", lambda t=t, b=b: [nc.tensor.matmul(
                bank(b), lhsT=xnT[:, c, t * 128:(t + 1) * 128], rhs=ws3[slot][:, c, :],
                start=(c == 0), stop=(c == 7)) for c in range(8)],
                reads=[win.tb[slot]], writes=[bankT[b]])
            src = bank(b).rearrange("p (h d) -> p h d", h=8)
            E_tick()
            if t % 2 == 0:
                P.op("dve", lambda t=t, src=src: nc.vector.tensor_copy(out=V[:, t, :, 0:64], in_=src),
                     reads=[bankT[b]])
            else:
                P.op("act", lambda t=t, src=src: nc.scalar.copy(out=V[:, t, :, 0:64], in_=src),
                     reads=[bankT[b]])

        E_flush()
        win.ensure(3)
        sC, sH, sB = win.slot(3), win.slot(4), win.slot(5)
        zc = [tmf32(2048 + k * 1640, 410) for k in range(2)]
        zbuf = tmf32(7168, 2050)
        acc = tmf32(15376, 2048)
        zc_t, zb_t, acc_t = [T(), T()], T(), T()
        k_it = 0
        bsel = 0
        for j in range(4):
            for pc in range(5):
                st = OWN0 - 1 + pc * 410
                C, H = (2 * bsel) % 8, (2 * bsel + 1) % 8
                bsel += 1
                P.op("pe", lambda j=j, st=st, C=C: [nc.tensor.matmul(
                    bank(C, 410), lhsT=ws3[sC][:, c, j * 128:(j + 1) * 128], rhs=xnT[:, c, st:st + 410],
                    start=(c == 0), stop=(c == 7)) for c in range(8)],
                    reads=[win.tb[sC]], writes=[bankT[C]])
                P.op("pe", lambda j=j, st=st, H=H: [nc.tensor.matmul(
                    bank(H, 410), lhsT=ws3[sH][:, c, j * 128:(j + 1) * 128], rhs=xnT[:, c, st:st + 410],
                    start=(c == 0), stop=(c == 7)) for c in range(8)],
                    reads=[win.tb[sH]], writes=[bankT[H]])
                kk = k_it % 2
                k_it += 1
                P.op("act", lambda kk=kk, C=C: nc.scalar.copy(out=zc[kk], in_=bank(C, 410)),
                     reads=[bankT[C]] + est_t, writes=[zc_t[kk]])
                P.op("dve", lambda kk=kk, H=H, pc=pc: nc.vector.tensor_tensor(
                    out=zbuf[:, pc * 410:(pc + 1) * 410], in0=bank(H, 410), in1=zc[kk], op=ALU.mult),
                    reads=[bankT[H], zc_t[kk]], writes=[zb_t])
            cw = lambda k, j=j: smp[:, 18 + 3 * j + k:19 + 3 * j + k]
            P.op("dve", lambda cw=cw: nc.vector.tensor_scalar(
                out=acc, in0=zbuf[:, 0:2048], scalar1=cw(0), scalar2=None, op0=ALU.mult),
                reads=[zb_t], writes=[acc_t])
            P.op("dve", lambda cw=cw: nc.vector.scalar_tensor_tensor(
                out=acc, in0=zbuf[:, 1:2049], scalar=cw(1), in1=acc, op0=ALU.mult, op1=ALU.add),
                reads=[zb_t], writes=[acc_t])
            P.op("dve", lambda cw=cw: nc.vector.scalar_tensor_tensor(
                out=acc, in0=zbuf[:, 2:2050], scalar=cw(2), in1=acc, op0=ALU.mult, op1=ALU.add),
                reads=[zb_t], writes=[acc_t])
            for tg in range(4):
                Bk = (2 * bsel) % 8
                bsel += 1
                P.op("pe", lambda j=j, tg=tg, Bk=Bk: [nc.tensor.matmul(
                    bank(Bk), lhsT=ws3[sB][:, c, j * 128:(j + 1) * 128],
                    rhs=xnT[:, c, OWN0 + tg * 512:OWN0 + (tg + 1) * 512],
                    start=(c == 0), stop=(c == 7)) for c in range(8)],
                    reads=[win.tb[sB]], writes=[bankT[Bk]])
                P.op("dve", lambda j=j, tg=tg, Bk=Bk: nc.vector.tensor_tensor(
                    out=BT[:, j, tg * 512:(tg + 1) * 512], in0=bank(Bk), in1=acc[:, tg * 512:(tg + 1) * 512],
                    op=ALU.mult), reads=[bankT[Bk], acc_t])
        P.barrier()
        if stop_after == "B":
            return _finish(nc, P, dumps, {"qT": U1[:, 32768:40960], "kT": U1[:, 20640:30880],
                                           "V": U1[:, 0:10400], "BT": U2[:, 28672:36864],
                                           "Egen": U1[:, 10400:15520]})

        gAf = [tm16(17408 + k * 6144, 1024).rearrange("p (c n) -> p c n", c=8) for k in range(2)]
        gBf = [tm16(17408 + k * 6144 + 2048, 1024).rearrange("p (c n) -> p c n", c=8) for k in range(2)]
        WAf = [tm16(17408 + k * 6144 + 4096, 512).rearrange("p (c n) -> p c n", c=4) for k in range(2)]
        WBf = [tm16(17408 + k * 6144 + 5120, 512).rearrange("p (c n) -> p c n", c=4) for k in range(2)]
        d1_t = [T(), T()]
        d1_ch = [P.chan("d1w0"), P.chan("d1w1")]

        def load_d1(f):
            k = f % 2
            wv = lambda w, c0: w[:, c0 + f * 128: c0 + (f + 1) * 128].rearrange("(c p) n -> p c n", p=128)
            P.dma_multi("pool", d1_ch[k], [(gAf[k], wv(w_in, 3072)), (gBf[k], wv(w_in, 4096)),
                                            (WAf[k], wv(w_a, 0)), (WBf[k], wv(w_b, 0))], writes=[d1_t[k]])

        load_d1(0)
        load_d1(1)
        w1s = Stream(P, "pool", "w1s", 3, ws3)
        for g in range(4):
            for s_ in range(8):
                w1s.add(w1[:, s_ * 512:(s_ + 1) * 512].rearrange("(c p) n -> p c n", p=128))

        pt = [tm16(7168 + k * 2560, 1280) for k in range(2)]
        pe_ = [WS[:, k, 0:1280] for k in range(3)]
        Abuf = [tm16(k * 1024, 512) for k in range(2)]
        rec = [stat[:, 8 * k:8 * k + 8] for k in range(2)]
        pt_t, pe_t = [T(), T()], [T(), T(), T()]
        A_t, rec_t = [T(), T()], [T(), T()]
        S_t = [T(), T()]
        PV_t = T()
        units = [(n, p) for n in PAIR_ORDER for p in range(4)]
        NU = len(units)

        def scores(u):
            n, p = units[u]
            base = (u % 2) * 1536
            P.op("pe", lambda: [nc.tensor.matmul(
                ps[:, base + hh * 640 + j * 128: base + hh * 640 + (j + 1) * 128],
                lhsT=kT[hh * 64:(hh + 1) * 64, p, (n + j) * 128:(n + j + 1) * 128],
                rhs=qT[hh * 64:(hh + 1) * 64, p, n * 128:(n + 1) * 128], start=True, stop=True)
                for j in range(5) for hh in range(2)], writes=[S_t[u % 2]])

        next_special = {0: 2, 1: 3, 14: 4}
        PVb_t = [T(), T()]
        pend_norm = None
        pend_evac = [None]
        scores(0)
        scores(1)

        def make_norm(u):
            n, p = units[u]
            pp = (u // 4) % 2
            sl = u % 2
            pvbase = (6 + sl) * 512

            def run():
                pvu = ps[:, pvbase:pvbase + 130].rearrange("p (h d) -> p h d", d=65)
                rec2 = stat[:, 2 * sl:2 * sl + 2]
                P.op("dve", lambda: nc.vector.reciprocal(out=rec2, in_=pvu[:, :, 64]),
                     reads=[PVb_t[sl]], writes=[rec_t[sl]])
                P.op("dve", lambda: nc.vector.tensor_tensor(
                    out=Abuf[pp][:, p * 128:(p + 1) * 128].rearrange("p (h d) -> p h d", h=2), in0=pvu[:, :, 0:64],
                    in1=rec2[:, :, None].broadcast_to([128, 2, 64]), op=ALU.mult),
                    reads=[PVb_t[sl], rec_t[sl]], writes=[A_t[pp]])
                if p == 3:
                    trv = bank16(7)[:, 512:1024]
                    P.op("pe", lambda: [nc.tensor.transpose(
                        trv[:, c * 128:(c + 1) * 128], Abuf[pp][:, c * 128:(c + 1) * 128], ident)
                        for c in range(4)], reads=[A_t[pp]], writes=[PVb_t[1]])

                    def evac():
                        P.op("dve", lambda: nc.vector.tensor_copy(
                            out=AT[:, :, n * 128:(n + 1) * 128],
                            in_=trv.rearrange("p (c n) -> p c n", c=4)), reads=[PVb_t[1]])
                    pend_evac[0] = evac
                    if n in next_special:
                        queue_E(next_special[n], Espc, "spc")
            return run

        for u in range(NU):
            n, p = units[u]
            sl = u % 2
            s3 = u % 3
            base = sl * 1536
            pvbase = (6 + sl) * 512
            if n in SPECIAL and p == 0:
                E_flush()
            else:
                E_tick()
            P.op("act", lambda: nc.scalar.activation(out=pt[sl], in_=ps[:, base:base + 1280], func=AF.Exp),
                 reads=[S_t[sl]], writes=[pt_t[sl]])
            if u + 2 < NU:
                scores(u + 2)
            if n in SPECIAL:
                Etab, Ekey = Espc, "spc"
            else:
                Etab, Ekey = Egen, "gen"
            P.op("dve", lambda: nc.vector.tensor_tensor(
                out=pe_[s3], in0=pt[sl], in1=Etab[:, 2 * p:2 * p + 2, :].rearrange("p a n -> p (a n)"),
                op=ALU.mult), reads=[pt_t[sl], E_t[Ekey]], writes=[pe_t[s3]])
            if pend_evac[0] is not None:
                pend_evac[0]()
                pend_evac[0] = None

            def pv_mms():
                out = []
                for hh in range(2):
                    h = 2 * p + hh
                    c0 = pvbase + hh * 65
                    for j in range(5):
                        out.append(nc.tensor.matmul(
                            ps[:, c0:c0 + 65], lhsT=pe_[s3][:, hh * 640 + j * 128: hh * 640 + (j + 1) * 128],
                            rhs=V[:, n + j, h, :], start=(j == 0), stop=(j == 4)))
                return out
            P.op("pe", pv_mms, reads=[pe_t[s3]], writes=[PVb_t[sl]])
            if pend_norm is not None:
                pend_norm()
            pend_norm = make_norm(u)
        pend_norm()
        if pend_evac[0] is not None:
            pend_evac[0]()
        P.barrier()
        if stop_after == "C":
            return _finish(nc, P, dumps, {"AT": U2[:, 20480:28672]})

        mergedT = U1[:, 0:16384].rearrange("p (c n) -> p c n", c=8)
        Wo = U1[:, 16384:24576].rearrange("p (c n) -> p c n", c=8)
        wo_t = T()
        wo_ch = P.chan("wo")
        P.dma_multi("pool", wo_ch, [(Wo[:, 2 * k:2 * k + 2, :],
                                     w_o[k * 256:(k + 1) * 256, :].rearrange("(c p) n -> p c n", p=128))
                                    for k in range(4)], writes=[wo_t])
        x1 = [u1f32(49152, 4096).rearrange("p (i d) -> p i d", i=4),
              tmf32(8192, 4096).rearrange("p (i d) -> p i d", i=4)]
        x1_t = [[T() for _ in range(4)] for _ in range(2)]
        x1_ch = [P.chan("x1a"), P.chan("x1b")]

        def x_load(g):
            bf = g % 2
            P.dma("sp", x1_ch[bf], x1[bf][:, :, :],
                  x_ext[OWN0 + g * 512: OWN0 + (g + 1) * 512, :].rearrange("(i p) d -> p i d", p=128),
                  writes=x1_t[bf])

        x_load(0)
        w1s.ensure(0)
        sga = [tmf32(k * 2048, 512) for k in range(2)]
        sgb = [tmf32(4096 + k * 2048, 512) for k in range(2)]
        t1 = [tmf32(8192 + k * 2048, 512) for k in range(2)]
        t2 = [tmf32(12288 + k * 2048, 512) for k in range(2)]
        sga_t, sgb_t, t1_t, t2_t = [T(), T()], [T(), T()], [T(), T()], [T(), T()]
        it = 0
        for f in range(8):
            k = f % 2
            for tg in range(4):
                b0 = (it % 2) * 4
                s = it % 2
                it += 1
                tok = slice(tg * 512, (tg + 1) * 512)
                xtok = slice(OWN0 + tg * 512, OWN0 + (tg + 1) * 512)
                P.op("pe", lambda: [nc.tensor.matmul(bank(b0), lhsT=gAf[k][:, c, :], rhs=xnT[:, c, xtok],
                                                     start=(c == 0), stop=(c == 7)) for c in range(8)],
                     reads=[d1_t[k]], writes=[bankT[b0]])
                P.op("pe", lambda: [nc.tensor.matmul(bank(b0 + 1), lhsT=WAf[k][:, c, :], rhs=AT[:, c, tok],
                                                     start=(c == 0), stop=(c == 3)) for c in range(4)],
                     reads=[d1_t[k]], writes=[bankT[b0 + 1]])
                P.op("pe", lambda: [nc.tensor.matmul(bank(b0 + 2), lhsT=gBf[k][:, c, :], rhs=xnT[:, c, xtok],
                                                     start=(c == 0), stop=(c == 7)) for c in range(8)],
                     reads=[d1_t[k]], writes=[bankT[b0 + 2]])
                P.op("pe", lambda: [nc.tensor.matmul(bank(b0 + 3), lhsT=WBf[k][:, c, :], rhs=BT[:, c, tok],
                                                     start=(c == 0), stop=(c == 3)) for c in range(4)],
                     reads=[d1_t[k]], writes=[bankT[b0 + 3]])
                P.op("act", lambda: nc.scalar.activation(out=sga[s], in_=bank(b0), func=AF.Sigmoid),
                     reads=[bankT[b0]], writes=[sga_t[s]])
                P.op("act", lambda: nc.scalar.activation(out=sgb[s], in_=bank(b0 + 2), func=AF.Sigmoid),
                     reads=[bankT[b0 + 2]], writes=[sgb_t[s]])
                P.op("dve", lambda: nc.vector.tensor_tensor(out=t1[s], in0=bank(b0 + 1), in1=sga[s], op=ALU.mult),
                     reads=[bankT[b0 + 1], sga_t[s]], writes=[t1_t[s]])
                P.op("dve", lambda: nc.vector.tensor_tensor(out=t2[s], in0=bank(b0 + 3), in1=sgb[s], op=ALU.mult),
                     reads=[bankT[b0 + 3], sgb_t[s]], writes=[t2_t[s]])
                P.op("dve", lambda: nc.vector.tensor_tensor(out=mergedT[:, f, tok], in0=t1[s], in1=t2[s], op=ALU.add),
                     reads=[t1_t[s], t2_t[s]])
            if f + 2 < 8:
                load_d1(f + 2)
        P.barrier()
        if stop_after == "D1":
            return _finish(nc, P, dumps, {"mergedT": U1[:, 0:16384]})

        xst2 = U1[:, 32768:36864].rearrange("p (i d) -> p i d", i=4)
        ot = [u1f32(73728 + k * 2048, 512) for k in range(4)]
        xn2T = [U2[:, 0:4096].rearrange("p (c n) -> p c n", c=8),
                U2[:, 32768:36864].rearrange("p (c n) -> p c n", c=8)]
        hT = U2[:, 4096:20480].rearrange("p (c n) -> p c n", c=32)
        w2s = [U2[:, 20480 + k * 4096:20480 + (k + 1) * 4096].rearrange("p (c n) -> p c n", c=8)
               for k in range(3)]
        rl = [tmf32(k * 2048, 512) for k in range(2)]
        junkE = tm16(4096, 1024)
        xn2c_t = [[T() for _ in range(8)] for _ in range(2)]
        hTc_t = [T() for _ in range(32)]
        xs2_t = [T() for _ in range(4)]
        t_junkE = T()
        rl_t = [T(), T()]
        ot_t = [T() for _ in range(4)]
        ot_ch = [P.chan(f"ot{k}") for k in range(4)]
        st2_t = [T(), T()]
        w2st = Stream(P, "pool", "w2s", 3, w2s)
        for g in range(4):
            for c2 in range(2):
                for sbk in range(4):
                    w2st.add(w2[sbk * 1024:(sbk + 1) * 1024, c2 * 512:(c2 + 1) * 512]
                             .rearrange("(c p) n -> p c n", p=128))
        w2st.ensure(0)
        gbc = [0]
        occ = [0]

        def nextbank():
            b_ = gbc[0] % 4
            gbc[0] += 1
            return b_

        def wo_part1(g):
            bf = g % 2
            for i in range(4):
                for c2 in range(2):
                    b_ = nextbank()
                    P.op("pe", lambda: [nc.tensor.matmul(
                        bank(b_), lhsT=mergedT[:, k, g * 512 + i * 128: g * 512 + (i + 1) * 128],
                        rhs=Wo[:, k, c2 * 512:(c2 + 1) * 512], start=(k == 0), stop=(k == 7)) for k in range(8)],
                        reads=[wo_t], writes=[bankT[b_]])
                    P.op("dve", lambda: nc.vector.tensor_tensor(
                        out=x1[bf][:, i, c2 * 512:(c2 + 1) * 512], in0=bank(b_),
                        in1=x1[bf][:, i, c2 * 512:(c2 + 1) * 512], op=ALU.add),
                        reads=[bankT[b_]], writes=[x1_t[bf][i]])
                P.op("act", lambda: nc.scalar.activation(
                    out=junkE, in_=x1[bf][:, i, :], func=AF.Square, scale=1.0 / 32.0,
                    accum_out=stat[:, 4 * bf + i:4 * bf + i + 1]),
                    reads=[x1_t[bf][i]], writes=[st2_t[bf], t_junkE])
            P.op("act", lambda: nc.scalar.activation(out=stat[:, 20 + 4 * bf:24 + 4 * bf],
                                                     in_=stat[:, 4 * bf:4 * bf + 4], func=AF.Ln, bias=EPS),
                 reads=[st2_t[bf]], writes=[st2_t[bf]])
            P.op("act", lambda: nc.scalar.activation(out=stat[:, 40 + 4 * bf:44 + 4 * bf],
                                                     in_=stat[:, 20 + 4 * bf:24 + 4 * bf], func=AF.Exp, scale=-0.5),
                 reads=[st2_t[bf]], writes=[st2_t[bf]])
            for i in range(4):
                P.op("dve", lambda: nc.vector.tensor_scalar(
                    out=xst2[:, i, :], in0=x1[bf][:, i, :], scalar1=stat[:, 40 + 4 * bf + i:41 + 4 * bf + i],
                    scalar2=None, op0=ALU.mult),
                    reads=[x1_t[bf][i], st2_t[bf]], writes=[xs2_t[i]])

        def wo_part2(g):
            bf = g % 2
            for c in range(8):
                b_ = nextbank()
                P.op("pe", lambda: [nc.tensor.transpose(
                    bank16(b_)[:, i * 128:(i + 1) * 128], xst2[:, i, c * 128:(c + 1) * 128], ident)
                    for i in range(4)], reads=xs2_t, writes=[bankT[b_]])
                if c % 2 == 0:
                    P.op("dve", lambda: nc.vector.tensor_scalar(
                        out=xn2T[bf][:, c, :], in0=bank16(b_)[:, 0:512], scalar1=smp[:, 8 + c:9 + c], scalar2=None,
                        op0=ALU.mult), reads=[bankT[b_]], writes=[xn2c_t[bf][c]])
                else:
                    P.op("act", lambda: nc.scalar.activation(
                        out=xn2T[bf][:, c, :], in_=bank16(b_)[:, 0:512], func=AF.Copy, scale=smp[:, 8 + c:9 + c]),
                        reads=[bankT[b_]], writes=[xn2c_t[bf][c]])

        def w2_half(g, c2, mid=None):
            bf = g % 2
            for sbk in range(4):
                li = g * 8 + c2 * 4 + sbk
                w2st.ensure(li)
                sl_ = w2st.slot(li)
                for i in range(4):
                    P.op("pe", lambda: [nc.tensor.matmul(
                        bank(4 + i), lhsT=hT[:, sbk * 8 + kk, i * 128:(i + 1) * 128], rhs=w2s[sl_][:, kk, :],
                        start=(sbk == 0 and kk == 0), stop=(sbk == 3 and kk == 7)) for kk in range(8)],
                        reads=[w2st.tb[sl_]] + hTc_t[sbk * 8:sbk * 8 + 8],
                        writes=[bankT[4 + i]] if (sbk == 3 or sbk == 0) else [])
                if sbk == 1 and mid is not None:
                    mid()
            for i in range(4):
                o = occ[0] % 4
                occ[0] += 1
                P.op("dve", lambda: nc.vector.tensor_tensor(
                    out=ot[o], in0=bank(4 + i), in1=x1[bf][:, i, c2 * 512:(c2 + 1) * 512], op=ALU.add),
                    reads=[bankT[4 + i], x1_t[bf][i]], writes=[ot_t[o]])
                P.dma("sp", ot_ch[o], out[g * 512 + i * 128: g * 512 + (i + 1) * 128, c2 * 512:(c2 + 1) * 512],
                      ot[o], reads=[ot_t[o]])

        wo_part1(0)
        wo_part2(0)
        for g in range(4):
            bf = g % 2
            if g + 1 < 4:
                x_load(g + 1)
            if g >= 1:
                w2st.issue_upto(8 * g + 2)
            for fc in range(32):
                li = g * 8 + fc // 4
                w1s.ensure(li)
                sl_ = w1s.slot(li)
                b_ = nextbank()
                P.op("pe", lambda: [nc.tensor.matmul(
                    bank(b_), lhsT=ws3[sl_][:, k, (fc % 4) * 128:(fc % 4 + 1) * 128], rhs=xn2T[bf][:, k, :],
                    start=(k == 0), stop=(k == 7)) for k in range(8)],
                    reads=[w1s.tb[sl_]] + xn2c_t[bf], writes=[bankT[b_]])
                r = fc % 2
                P.op("act", lambda: nc.scalar.activation(out=rl[r], in_=bank(b_), func=AF.Relu),
                     reads=[bankT[b_]], writes=[rl_t[r]])
                P.op("dve", lambda: nc.vector.tensor_tensor(out=hT[:, fc, :], in0=rl[r], in1=rl[r], op=ALU.mult),
                     reads=[rl_t[r]], writes=[hTc_t[fc]])
            if g + 1 < 4:
                w1s.issue_upto(8 * (g + 1) + 2)
            w2_half(g, 0)
            if g + 1 < 4:
                wo_part1(g + 1)
                w2_half(g, 1, mid=lambda: wo_part2(g + 1))
            else:
                w2_half(g, 1)
        P.barrier()
        return _finish(nc, P, dumps, {})
    return nc


def _finish(nc, P, dumps, avail):
    if dumps:
        ch = P.chan("dbg")
        for name, ap in dumps.items():
            P.dma("sp", ch, ap[:, :], avail[name])
        P.barrier()
    return nc


def _ext_tile_rows(q, e):
    R0 = 32 * q
    l0 = 2 * e - 4
    if q == 0 and e == 0:
        return (6, 7)
    if q == 0 and e == 1:
        return (None, None)
    if q == 3 and e == 18:
        return (None, None)
    if q == 3 and e == 19:
        return (R0 + 24, R0 + 25)
    return (R0 + l0, R0 + l0 + 1)


def _bias_table(rpb, q, n):
    R0 = 32 * q
    kc = np.arange(64)[:, None]
    qc = np.arange(64)[None, :]
    wst = np.clip(qc - 8, 0, 48)
    colmask = (kc >= wst) & (kc < wst + 16)
    relc = np.clip(kc - qc + 15, 0, 30)
    tab = np.full((2, 64, 8, 5, 2, 64), NEG, np.float32)
    for j in range(5):
        rows = _ext_tile_rows(q, n + j)
        for krl in range(2):
            kr = rows[krl]
            if kr is None:
                continue
            for qrl in range(2):
                r = R0 + 2 * n + qrl
                rs = min(max(r - 4, 0), 120)
                if not (rs <= kr < rs + 8):
                    continue
                blk = rpb[:, kr - r + 7, :][:, relc]
                blk = np.where(colmask[None], blk, np.float32(NEG))
                tab[krl, :, :, j, qrl, :] = blk.transpose(1, 0, 2)
    return tab.reshape(128, 8 * 5 * 128)


def _host_inputs(x, norm1_g, w_in, q_norm_g, k_norm_g, rpb, conv_w, w_attn_branch,
                 w_conv_branch, w_o, norm2_g, w_mlp_in, w_mlp_out):
    f = lambda a: np.ascontiguousarray(np.asarray(a, dtype=np.float32))
    x = f(x)
    rpb0 = f(rpb)[0]
    smallp = np.zeros((128, 32), np.float32)
    smallp[:, 0:8] = f(norm1_g)[0].reshape(8, 128).T
    smallp[:, 8:16] = f(norm2_g)[0].reshape(8, 128).T
    smallp[:, 16] = np.tile(f(q_norm_g)[0], 2)
    smallp[:, 17] = np.tile(f(k_norm_g)[0], 2)
    smallp[:, 18:30] = f(conv_w)[0].reshape(3, 4, 128).transpose(2, 1, 0).reshape(128, 12)
    consts = np.zeros((128, 256), np.float32)
    consts[:, 0:128] = np.eye(128, dtype=np.float32)
    blk = np.arange(128) // 64
    consts[:, 128:256] = (blk[:, None] == blk[None, :]).astype(np.float32)
    shared = {
        "smallp": smallp, "consts": consts, "w_in": f(w_in)[0], "w_a": f(w_attn_branch)[0],
        "w_b": f(w_conv_branch)[0], "w_o": f(w_o)[0], "w1": f(w_mlp_in)[0], "w2": f(w_mlp_out)[0],
    }
    tabs = {}
    in_maps = []
    for core in range(N_CORES):
        b, q = core // 4, core % 4
        xg = x[b].reshape(128, 64, D)
        xe = np.zeros((40, 64, D), np.float32)
        for e in range(20):
            rows = _ext_tile_rows(q, e)
            for k in range(2):
                if rows[k] is not None and 0 <= rows[k] < 128:
                    xe[2 * e + k] = xg[rows[k]]
        if q not in tabs:
            tabs[q] = np.stack([_bias_table(rpb0, q, n) for n in (5, 0, 1, 14, 15)], axis=0)
        m = dict(shared)
        m["x_ext"] = xe.reshape(TEXT, D)
        m["btab"] = tabs[q]
        in_maps.append(m)
    return in_maps


_NC_CACHE = {}


def kernel(**inputs):
    in_maps = _host_inputs(**inputs)
    if "nc" not in _NC_CACHE:
        _NC_CACHE["nc"] = build_nc()
    res = run_bass_kernel_spmd(_NC_CACHE["nc"], in_maps, core_ids=list(range(N_CORES)))
    outs = [np.asarray(r["out"], dtype=np.float32).reshape(TOK, D) for r in res.results]
    full = np.concatenate(outs, axis=0).reshape(2, 8192, D)
    return full
```

```python
import contextlib
import numpy as np
import concourse.bass as bass
import concourse.mybir as mybir
from concourse.bass_utils import run_bass_kernel_spmd

F32 = mybir.dt.float32
BF16 = mybir.dt.bfloat16
ALU = mybir.AluOpType
AF = mybir.ActivationFunctionType

N_CORES = 8
D = 1024
PROJ = 5120
DFF = 4096
TOK = 2048
TEXT = 2560
OWN0 = 256
NEG = -30000.0
EPS = 1e-6
PAIR_ORDER = [0, 2, 3, 4, 5, 1, 6, 7, 8, 9, 14, 10, 11, 12, 13, 15]
SPECIAL = {0: 1, 1: 2, 14: 3, 15: 4}


class T:
    def __init__(self):
        self.w = None
        self.r = {}


class Prog:
    def __init__(self, nc, es):
        self.nc = nc
        self.es = es
        self.eng = {"pe": nc.tensor, "act": nc.scalar, "dve": nc.vector,
                    "pool": nc.gpsimd, "sp": nc.sync}
        self.sem = {}
        self.cnt = {}
        self.seen = {}
        self.chans = []
        for n in ("pe", "act", "dve"):
            self.sem[n] = es.enter_context(nc.semaphore("s_" + n))
            self.cnt[n] = 0

    def chan(self, name):
        self.sem[name] = self.es.enter_context(self.nc.semaphore("c_" + name))
        self.cnt[name] = 0
        self.chans.append(name)
        return name

    def wait(self, e, *toks):
        for t in toks:
            if t is None:
                continue
            n, v = t
            if self.seen.get((e, n), 0) < v:
                self.eng[e].wait_ge(self.sem[n], v)
                self.seen[(e, n)] = v

    def _deps(self, reads, writes):
        toks = []
        for b in reads:
            toks.append(b.w)
        for b in writes:
            toks.append(b.w)
            toks.extend(b.r.items())
        return toks

    def _commit(self, tok, reads, writes):
        for b in reads:
            if b.r.get(tok[0], 0) < tok[1]:
                b.r[tok[0]] = tok[1]
        for b in writes:
            b.w = tok
            b.r = {}

    def op(self, e, make, reads=(), writes=(), extra=()):
        self.wait(e, *self._deps(reads, writes), *extra)
        inst = make()
        if isinstance(inst, (list, tuple)):
            inst = inst[-1]
        inst.then_inc(self.sem[e], 1)
        self.cnt[e] += 1
        tok = (e, self.cnt[e])
        self._commit(tok, reads, writes)
        return tok

    def dma(self, q, ch, out, in_, reads=(), writes=(), extra=()):
        self.wait(q, *self._deps(reads, writes), *extra)
        inst = self.eng[q].dma_start(out=out, in_=in_)
        inst.then_inc(self.sem[ch], 16)
        self.cnt[ch] += 16
        tok = (ch, self.cnt[ch])
        self._commit(tok, reads, writes)
        return tok

    def dma_multi(self, q, ch, pairs, reads=(), writes=()):
        self.wait(q, *self._deps(reads, writes))
        for out, in_ in pairs:
            inst = self.eng[q].dma_start(out=out, in_=in_)
            inst.then_inc(self.sem[ch], 16)
            self.cnt[ch] += 16
        tok = (ch, self.cnt[ch])
        self._commit(tok, reads, writes)
        return tok

    def barrier(self):
        toks = [(n, self.cnt[n]) for n in ("pe", "act", "dve") + tuple(self.chans)
                if self.cnt[n] > 0]
        for e in self.eng:
            self.wait(e, *toks)


class Stream:
    def __init__(self, P, q, name, nslots, slot_aps):
        self.P = P
        self.q = q
        self.n = nslots
        self.aps = slot_aps
        self.tb = [T() for _ in range(nslots)]
        self.ch = [P.chan(f"{name}{i}") for i in range(nslots)]
        self.srcs = []
        self.issued = 0

    def add(self, src):
        self.srcs.append(src)
        return len(self.srcs) - 1

    def ensure(self, i, extra=()):
        self.issue_upto(i + self.n - 1, extra)

    def issue_upto(self, last, extra=()):
        while self.issued <= min(last, len(self.srcs) - 1):
            k = self.issued
            s = k % self.n
            self.P.dma(self.q, self.ch[s], self.aps[s], self.srcs[k], writes=[self.tb[s]], extra=extra)
            self.issued += 1

    def slot(self, i):
        return i % self.n


def build_nc(stop_after=None, dump=()):
    nc = bass.Bass("TRN2", target_bir_lowering=False)
    x_ext = nc.dram_tensor("x_ext", [TEXT, D], F32, kind="ExternalInput").ap()
    btab = nc.dram_tensor("btab", [5, 128, 5120], F32, kind="ExternalInput").ap()
    smallp = nc.dram_tensor("smallp", [128, 32], F32, kind="ExternalInput").ap()
    consts = nc.dram_tensor("consts", [128, 256], F32, kind="ExternalInput").ap()
    w_in = nc.dram_tensor("w_in", [D, PROJ], F32, kind="ExternalInput").ap()
    w_a = nc.dram_tensor("w_a", [512, D], F32, kind="ExternalInput").ap()
    w_b = nc.dram_tensor("w_b", [512, D], F32, kind="ExternalInput").ap()
    w_o = nc.dram_tensor("w_o", [D, D], F32, kind="ExternalInput").ap()
    w1 = nc.dram_tensor("w1", [D, DFF], F32, kind="ExternalInput").ap()
    w2 = nc.dram_tensor("w2", [DFF, D], F32, kind="ExternalInput").ap()
    out = nc.dram_tensor("out", [TOK, D], F32, kind="ExternalOutput").ap()
    dumps = {}
    for name, shape, dt in dump:
        dumps[name] = nc.dram_tensor("dbg_" + name, list(shape), dt, kind="ExternalOutput").ap()

    with contextlib.ExitStack() as es:
        P = Prog(nc, es)
        sb = lambda name, shape, dt: es.enter_context(nc.sbuf_tensor(name, shape, dt))
        U1 = sb("u1", [128, 40960], BF16)
        U2 = sb("u2", [128, 36864], BF16)
        WS = sb("ws", [128, 3, 4096], BF16)
        TM = sb("tm", [128, 14848], BF16)
        smp = sb("smp", [128, 32], F32)
        gs = sb("gs", [128, 2], F32)
        cf = sb("cf", [128, 256], F32)
        cb16 = sb("cb16", [128, 256], BF16)
        stat = sb("stat", [128, 64], F32)
        ps = es.enter_context(nc.psum_tensor("ps", [128, 4096], F32))
        ident = cb16[:, 0:128]
        bones = cb16[:, 128:256]

        def bank(b, n=512):
            return ps[:, b * 512:b * 512 + n]

        def bank16(b):
            return ps[:, b * 512:(b + 1) * 512].bitcast(BF16)

        bankT = [T() for _ in range(8)]

        def u1f32(off_b, n):
            return U1[:, off_b // 2: off_b // 2 + 2 * n].bitcast(F32)

        def tmf32(off_b, n):
            return TM[:, off_b // 2: off_b // 2 + 2 * n].bitcast(F32)

        def tm16(off_b, n):
            return TM[:, off_b // 2: off_b // 2 + n]

        xnT = U2[:, 0:20480].rearrange("p (c n) -> p c n", c=8)
        AT = U2[:, 20480:28672].rearrange("p (c n) -> p c n", c=4)
        BT = U2[:, 28672:36864].rearrange("p (c n) -> p c n", c=4)
        V = U1[:, 0:10400].rearrange("p (t h d) -> p t h d", t=20, h=8)
        Egen = U1[:, 10400:15520].rearrange("p (h n) -> p h n", h=8)
        Espc = U1[:, 15520:20640].rearrange("p (h n) -> p h n", h=8)
        kT = U1[:, 20640:30880].rearrange("p (c n) -> p c n", c=4)
        qT = U1[:, 32768:40960].rearrange("p (c n) -> p c n", c=4)
        ws3 = [WS[:, s, :].rearrange("p (c n) -> p c n", c=8) for s in range(3)]

        c_small = P.chan("small")
        t_small = T()
        P.dma_multi("sp", c_small, [(smp[:], smallp[:, :]), (cf[:], consts[:, :])], writes=[t_small])
        t_c16 = T()
        P.op("dve", lambda: nc.vector.tensor_copy(out=cb16[:], in_=cf[:]), reads=[t_small], writes=[t_c16])
        t_gs = T()
        P.op("dve", lambda: [
            nc.vector.tensor_scalar(out=gs[:, 0:1], in0=smp[:, 16:17], scalar1=0.125, scalar2=None, op0=ALU.mult),
            nc.vector.tensor_copy(out=gs[:, 1:2], in_=smp[:, 17:18]),
            nc.vector.memset(stat[:], 0.0)], reads=[t_small], writes=[t_gs])

        win = Stream(P, "pool", "win", 3, ws3)
        for s in (1, 0, 2, 4, 5, 3):
            win.add(w_in[:, s * 512:(s + 1) * 512].rearrange("(c p) n -> p c n", p=128))

        NXS = 8
        xin = [u1f32(k * 4096, 1024) for k in range(NXS)]
        xin_t = [T() for _ in range(NXS)]
        xin_ch = [P.chan(f"xin{k}") for k in range(NXS)]
        xnT_t = [[T() for _ in range(8)] for _ in range(5)]
        xst = [U1[:, 16384:20480].rearrange("p (i d) -> p i d", i=4),
               tm16(17408, 4096).rearrange("p (i d) -> p i d", i=4)]
        xst_t = [[T() for _ in range(4)] for _ in range(2)]
        junk = tm16(0, 1024)
        t_junk = T()
        ms_t = [T() for _ in range(5)]
        rs_t = [T() for _ in range(5)]

        def load_x(t):
            P.dma("sp", xin_ch[t % NXS], xin[t % NXS], x_ext[t * 128:(t + 1) * 128, :], writes=[xin_t[t % NXS]])

        for t in range(NXS):
            load_x(t)
        evc = [0]

        def stage1a(g, i):
            t = 4 * g + i
            P.op("act", lambda: nc.scalar.activation(
                out=junk, in_=xin[t % NXS], func=AF.Square, scale=1.0 / 32.0,
                accum_out=stat[:, t:t + 1]),
                reads=[xin_t[t % NXS], t_gs], writes=[ms_t[g], t_junk] if i == 3 else [t_junk])

        def stage1b(g):
            P.op("act", lambda: nc.scalar.activation(
                out=stat[:, 20 + 4 * g:24 + 4 * g], in_=stat[:, 4 * g:4 * g + 4], func=AF.Ln, bias=EPS),
                reads=[ms_t[g]], writes=[rs_t[g]])
            P.op("act", lambda: nc.scalar.activation(
                out=stat[:, 40 + 4 * g:44 + 4 * g], in_=stat[:, 20 + 4 * g:24 + 4 * g], func=AF.Exp, scale=-0.5),
                reads=[rs_t[g]], writes=[rs_t[g]])
            for i in range(4):
                t = 4 * g + i
                P.op("dve", lambda: nc.vector.tensor_scalar(
                    out=xst[g % 2][:, i, :], in0=xin[t % NXS], scalar1=stat[:, 40 + t:41 + t],
                    scalar2=None, op0=ALU.mult),
                    reads=[xin_t[t % NXS], rs_t[g]], writes=[xst_t[g % 2][i]])
                if t + NXS < 20:
                    load_x(t + NXS)

        def stage1(g):
            for i in range(4):
                stage1a(g, i)
            stage1b(g)

        def stage2(g):
            for c in range(8):
                b_ = evc[0] % 4
                evc[0] += 1
                P.op("pe", lambda: [
                    nc.tensor.transpose(bank16(b_)[:, i * 128:(i + 1) * 128],
                                        xst[g % 2][:, i, c * 128:(c + 1) * 128], ident)
                    for i in range(4)],
                    reads=xst_t[g % 2] + [t_c16], writes=[bankT[b_]])
                dst = xnT[:, c, g * 512:(g + 1) * 512]
                if c % 4 != 3:
                    P.op("dve", lambda: nc.vector.tensor_scalar(
                        out=dst, in0=bank16(b_)[:, 0:512], scalar1=smp[:, c:c + 1], scalar2=None,
                        op0=ALU.mult), reads=[bankT[b_]], writes=[xnT_t[g][c]])
                else:
                    P.op("act", lambda: nc.scalar.activation(
                        out=dst, in_=bank16(b_)[:, 0:512], func=AF.Copy, scale=smp[:, c:c + 1]),
                        reads=[bankT[b_]], writes=[xnT_t[g][c]])

        NEST = 4
        est = [tmf32(2048, 640), tmf32(4608, 640), tmf32(12288, 640), tmf32(14848, 640)]
        est_t = [T() for _ in range(NEST)]
        est_ch = [P.chan(f"est{k}") for k in range(NEST)]
        E_t = {"gen": T(), "spc": T()}
        esteps = []
        edma = [0]
        edone = [0]

        def E_dma_upto(last):
            while edma[0] <= min(last, len(esteps) - 1):
                kk = edma[0]
                v_, h_, _, _ = esteps[kk]
                P.dma("sp", est_ch[kk % NEST], est[kk % NEST], btab[v_, :, h_ * 640:(h_ + 1) * 640],
                      writes=[est_t[kk % NEST]])
                edma[0] += 1

        def queue_E(variant, dst, key, prefetch=True):
            for h in range(8):
                esteps.append((variant, h, dst, key))
            if prefetch:
                E_dma_upto(edone[0] + NEST - 1)

        def E_tick(nsteps=1):
            for _ in range(nsteps):
                k = edone[0]
                if k >= len(esteps):
                    return
                E_dma_upto(k + NEST - 1)
                v_, h_, dst, key = esteps[k]
                P.op("act", lambda: nc.scalar.activation(out=dst[:, h_, :], in_=est[k % NEST], func=AF.Exp),
                     reads=[est_t[k % NEST]], writes=[E_t[key]])
                edone[0] += 1

        def E_flush():
            E_tick(len(esteps))

        queue_E(0, Egen, "gen", prefetch=False)
        queue_E(1, Espc, "spc", prefetch=False)

        sq = [tm16(7168 + k * 1024, 512) for k in range(2)]
        lnt = [tmf32(9216 + k * 2048, 512) for k in range(2)]
        rsb = [tmf32(13312 + k * 2048, 512) for k in range(2)]
        sq_t, ln_t, rb_t = [T(), T()], [T(), T()], [T(), T()]

        def qk_items(seq):
            def proj(n, it):
                slot, p, tok0, deps, dst, gcol = it
                X = (2 * n) % 8
                P.op("pe", lambda: [nc.tensor.matmul(
                    bank(X), lhsT=ws3[slot][:, c, p * 128:(p + 1) * 128],
                    rhs=xnT[:, c, tok0: tok0 + 512],
                    start=(c == 0), stop=(c == 7)) for c in range(8)],
                    reads=[win.tb[slot]] + deps, writes=[bankT[X]])
                P.op("act", lambda: nc.scalar.activation(out=sq[n % 2], in_=bank(X), func=AF.Square),
                     reads=[bankT[X]], writes=[sq_t[n % 2]])

            def rest(n, it):
                slot, p, tok0, deps, dst, gcol = it
                X, Y = (2 * n) % 8, (2 * n + 1) % 8
                P.op("pe", lambda: nc.tensor.matmul(bank(Y), lhsT=bones, rhs=sq[n % 2], start=True, stop=True),
                     reads=[sq_t[n % 2]], writes=[bankT[Y]])
                P.op("act", lambda: nc.scalar.activation(out=lnt[n % 2], in_=bank(Y), func=AF.Ln,
                                                         scale=1.0 / 64.0, bias=EPS),
                     reads=[bankT[Y]], writes=[ln_t[n % 2]])
                P.op("act", lambda: nc.scalar.activation(out=rsb[n % 2], in_=lnt[n % 2], func=AF.Exp, scale=-0.5),
                     reads=[ln_t[n % 2]], writes=[rb_t[n % 2]])
                P.op("dve", lambda: nc.vector.scalar_tensor_tensor(
                    out=dst, in0=bank(X), scalar=gs[:, gcol:gcol + 1],
                    in1=rsb[n % 2], op0=ALU.mult, op1=ALU.mult),
                    reads=[bankT[X], rb_t[n % 2]])

            prev = None
            n = 0
            for e in seq:
                if callable(e):
                    e()
                    continue
                proj(n, e)
                if prev is not None:
                    rest(*prev)
                prev = (n, e)
                n += 1
            rest(*prev)

        sK, sQ = win.slot(0), win.slot(1)

        def K_items(eg):
            return [(sK, p, eg * 512, list(xnT_t[eg]), kT[:, p, eg * 512:(eg + 1) * 512], 1) for p in range(4)]

        def Q_items(tg):
            return [(sQ, p, OWN0 + tg * 512, xnT_t[tg] + xnT_t[tg + 1], qT[:, p, tg * 512:(tg + 1) * 512], 0)
                    for p in range(4)]

        win.issue_upto(0, extra=[xin_t[3].w])
        win.issue_upto(1, extra=[xin_t[7].w])
        stage1(0)
        stage1(1)
        stage2(0)
        def weave(items, g):
            out = []
            for i, it in enumerate(items):
                out.append(it)
                out.append(lambda g=g, i=i: stage1a(g, i))
            return out

        seq = weave(K_items(0), 2)
        seq += [lambda: (stage1b(2), win.issue_upto(2, extra=[xin_t[19 % NXS].w]), stage2(1))]
        seq += Q_items(0)
        seq += weave(K_items(1), 3)
        seq += [lambda: (stage1b(3), stage2(2))]
        seq += weave(Q_items(1), 4)
        seq += [lambda: stage1b(4)]
        seq += K_items(2)
        seq += [lambda: stage2(3)]
        seq += Q_items(2) + K_items(3)
        seq += [lambda: stage2(4)]
        seq += Q_items(3) + K_items(4)
        qk_items(seq)
        P.op("dve", lambda: nc.vector.memset(V[:, :, :, 64:65], 1.0))

        for k_ in (2, 3):
            est_t[k_].r = {"act": P.cnt["act"], "dve": P.cnt["dve"], "pe": P.cnt["pe"]}
        win.ensure(2)
        slot = win.slot(2)
        for t in range(20):
            b = t % 8
            P.op("pe", lambda t=t, b=b: [nc.tensor.matmul(
                bank(b), lhsT=xnT[:, c, t * 128:(t + 1) * 128], rhs=ws3[slot][:, c, :],
                start=(c == 0), stop=(c == 7)) for c in range(8)],
                reads=[win.tb[slot]] + xnT_t[t // 4], writes=[bankT[b]])
            src = bank(b).rearrange("p (h d) -> p h d", h=8)
            E_tick()
            if t % 2 == 0:
                P.op("dve", lambda t=t, src=src: nc.vector.tensor_copy(out=V[:, t, :, 0:64], in_=src),
                     reads=[bankT[b]])
            else:
                P.op("act", lambda t=t, src=src: nc.scalar.copy(out=V[:, t, :, 0:64], in_=src),
                     reads=[bankT[b]])

        E_flush()
        win.ensure(3)
        sC, sH, sB = win.slot(3), win.slot(4), win.slot(5)
        zc = [tmf32(2048 + k * 1640, 410) for k in range(2)]
        zbuf = tmf32(7168, 2050)
        acc = tmf32(15376, 2048)
        zc_t, zb_t, acc_t = [T(), T()], T(), T()
        k_it = 0
        bsel = 0
        for j in range(4):
            for pc in range(5):
                st = OWN0 - 1 + pc * 410
                C, H = (2 * bsel) % 8, (2 * bsel + 1) % 8
                bsel += 1
                P.op("pe", lambda j=j, st=st, C=C: [nc.tensor.matmul(
                    bank(C, 410), lhsT=ws3[sC][:, c, j * 128:(j + 1) * 128], rhs=xnT[:, c, st:st + 410],
                    start=(c == 0), stop=(c == 7)) for c in range(8)],
                    reads=[win.tb[sC]], writes=[bankT[C]])
                P.op("pe", lambda j=j, st=st, H=H: [nc.tensor.matmul(
                    bank(H, 410), lhsT=ws3[sH][:, c, j * 128:(j + 1) * 128], rhs=xnT[:, c, st:st + 410],
                    start=(c == 0), stop=(c == 7)) for c in range(8)],
                    reads=[win.tb[sH]], writes=[bankT[H]])
                kk = k_it % 2
                k_it += 1
                P.op("act", lambda kk=kk, C=C: nc.scalar.copy(out=zc[kk], in_=bank(C, 410)),
                     reads=[bankT[C]] + est_t, writes=[zc_t[kk]])
                P.op("dve", lambda kk=kk, H=H, pc=pc: nc.vector.tensor_tensor(
                    out=zbuf[:, pc * 410:(pc + 1) * 410], in0=bank(H, 410), in1=zc[kk], op=ALU.mult),
                    reads=[bankT[H], zc_t[kk]], writes=[zb_t])
            cw = lambda k, j=j: smp[:, 18 + 3 * j + k:19 + 3 * j + k]
            P.op("dve", lambda cw=cw: nc.vector.tensor_scalar(
                out=acc, in0=zbuf[:, 0:2048], scalar1=cw(0), scalar2=None, op0=ALU.mult),
                reads=[zb_t], writes=[acc_t])
            P.op("dve", lambda cw=cw: nc.vector.scalar_tensor_tensor(
                out=acc, in0=zbuf[:, 1:2049], scalar=cw(1), in1=acc, op0=ALU.mult, op1=ALU.add),
                reads=[zb_t], writes=[acc_t])
            P.op("dve", lambda cw=cw: nc.vector.scalar_tensor_tensor(
                out=acc, in0=zbuf[:, 2:2050], scalar=cw(2), in1=acc, op0=ALU.mult, op1=ALU.add),
                reads=[zb_t], writes=[acc_t])
            for tg in range(4):
                Bk = (2 * bsel) % 8
                bsel += 1
                P.op("pe", lambda j=j, tg=tg, Bk=Bk: [nc.tensor.matmul(
                    bank(Bk), lhsT=ws3[sB][:, c, j * 128:(j + 1) * 128],
                    rhs=xnT[:, c, OWN0 + tg * 512:OWN0 + (tg + 1) * 512],
                    start=(c == 0), stop=(c == 7)) for c in range(8)],
                    reads=[win.tb[sB]], writes=[bankT[Bk]])
                P.op("dve", lambda j=j, tg=tg, Bk=Bk: nc.vector.tensor_tensor(
                    out=BT[:, j, tg * 512:(tg + 1) * 512], in0=bank(Bk), in1=acc[:, tg * 512:(tg + 1) * 512],
                    op=ALU.mult), reads=[bankT[Bk], acc_t])
        P.barrier()
        if stop_after == "B":
            return _finish(nc, P, dumps, {"qT": U1[:, 32768:40960], "kT": U1[:, 20640:30880],
                                           "V": U1[:, 0:10400], "BT": U2[:, 28672:36864],
                                           "Egen": U1[:, 10400:15520]})

        gAf = [tm16(17408 + k * 6144, 1024).rearrange("p (c n) -> p c n", c=8) for k in range(2)]
        gBf = [tm16(17408 + k * 6144 + 2048, 1024).rearrange("p (c n) -> p c n", c=8) for k in range(2)]
        WAf = [tm16(17408 + k * 6144 + 4096, 512).rearrange("p (c n) -> p c n", c=4) for k in range(2)]
        WBf = [tm16(17408 + k * 6144 + 5120, 512).rearrange("p (c n) -> p c n", c=4) for k in range(2)]
        d1_t = [T(), T()]
        d1_ch = [P.chan("d1w0"), P.chan("d1w1")]

        def load_d1(f):
            k = f % 2
            wv = lambda w, c0: w[:, c0 + f * 128: c0 + (f + 1) * 128].rearrange("(c p) n -> p c n", p=128)
            P.dma_multi("pool", d1_ch[k], [(gAf[k], wv(w_in, 3072)), (gBf[k], wv(w_in, 4096)),
                                            (WAf[k], wv(w_a, 0)), (WBf[k], wv(w_b, 0))], writes=[d1_t[k]])

        load_d1(0)
        load_d1(1)
        w1s = Stream(P, "pool", "w1s", 3, ws3)
        for g in range(4):
            for s_ in range(8):
                w1s.add(w1[:, s_ * 512:(s_ + 1) * 512].rearrange("(c p) n -> p c n", p=128))

        pt = [tm16(7168 + k * 2560, 1280) for k in range(2)]
        pe_ = [WS[:, k, 0:1280] for k in range(3)]
        Abuf = [tm16(k * 1024, 512) for k in range(2)]
        rec = [stat[:, 8 * k:8 * k + 8] for k in range(2)]
        pt_t, pe_t = [T(), T()], [T(), T(), T()]
        A_t, rec_t = [T(), T()], [T(), T()]
        S_t = [T(), T()]
        PV_t = T()
        units = [(n, p) for n in PAIR_ORDER for p in range(4)]
        NU = len(units)

        def scores(u):
            n, p = units[u]
            base = (u % 2) * 1536
            P.op("pe", lambda: [nc.tensor.matmul(
                ps[:, base + hh * 640 + j * 128: base + hh * 640 + (j + 1) * 128],
                lhsT=kT[hh * 64:(hh + 1) * 64, p, (n + j) * 128:(n + j + 1) * 128],
                rhs=qT[hh * 64:(hh + 1) * 64, p, n * 128:(n + 1) * 128], start=True, stop=True)
                for j in range(5) for hh in range(2)], writes=[S_t[u % 2]])

        next_special = {0: 2, 1: 3, 14: 4}
        PVb_t = [T(), T()]
        pend_norm = None
        pend_evac = [None]
        scores(0)
        scores(1)

        def make_norm(u):
            n, p = units[u]
            pp = (u // 4) % 2
            sl = u % 2
            pvbase = (6 + sl) * 512

            def run():
                pvu = ps[:, pvbase:pvbase + 130].rearrange("p (h d) -> p h d", d=65)
                rec2 = stat[:, 2 * sl:2 * sl + 2]
                P.op("dve", lambda: nc.vector.reciprocal(out=rec2, in_=pvu[:, :, 64]),
                     reads=[PVb_t[sl]], writes=[rec_t[sl]])
                P.op("dve", lambda: nc.vector.tensor_tensor(
                    out=Abuf[pp][:, p * 128:(p + 1) * 128].rearrange("p (h d) -> p h d", h=2), in0=pvu[:, :, 0:64],
                    in1=rec2[:, :, None].broadcast_to([128, 2, 64]), op=ALU.mult),
                    reads=[PVb_t[sl], rec_t[sl]], writes=[A_t[pp]])
                if p == 3:
                    trv = bank16(7)[:, 512:1024]
                    P.op("pe", lambda: [nc.tensor.transpose(
                        trv[:, c * 128:(c + 1) * 128], Abuf[pp][:, c * 128:(c + 1) * 128], ident)
                        for c in range(4)], reads=[A_t[pp]], writes=[PVb_t[1]])

                    def evac():
                        P.op("dve", lambda: nc.vector.tensor_copy(
                            out=AT[:, :, n * 128:(n + 1) * 128],
                            in_=trv.rearrange("p (c n) -> p c n", c=4)), reads=[PVb_t[1]])
                    pend_evac[0] = evac
                    if n in next_special:
                        queue_E(next_special[n], Espc, "spc")
            return run

        for u in range(NU):
            n, p = units[u]
            sl = u % 2
            s3 = u % 3
            base = sl * 1536
            pvbase = (6 + sl) * 512
            if n in SPECIAL and p == 0:
                E_flush()
            else:
                E_tick()
            P.op("act", lambda: nc.scalar.activation(out=pt[sl], in_=ps[:, base:base + 1280], func=AF.Exp),
                 reads=[S_t[sl]], writes=[pt_t[sl]])
            if u + 2 < NU:
                scores(u + 2)
            if n in SPECIAL:
                Etab, Ekey = Espc, "spc"
            else:
                Etab, Ekey = Egen, "gen"
            P.op("dve", lambda: nc.vector.tensor_tensor(
                out=pe_[s3], in0=pt[sl], in1=Etab[:, 2 * p:2 * p + 2, :].rearrange("p a n -> p (a n)"),
                op=ALU.mult), reads=[pt_t[sl], E_t[Ekey]], writes=[pe_t[s3]])
            if pend_evac[0] is not None:
                pend_evac[0]()
                pend_evac[0] = None

            def pv_mms():
                out = []
                for hh in range(2):
                    h = 2 * p + hh
                    c0 = pvbase + hh * 65
                    for j in range(5):
                        out.append(nc.tensor.matmul(
                            ps[:, c0:c0 + 65], lhsT=pe_[s3][:, hh * 640 + j * 128: hh * 640 + (j + 1) * 128],
                            rhs=V[:, n + j, h, :], start=(j == 0), stop=(j == 4)))
                return out
            P.op("pe", pv_mms, reads=[pe_t[s3]], writes=[PVb_t[sl]])
            if pend_norm is not None:
                pend_norm()
            pend_norm = make_norm(u)
        pend_norm()
        if pend_evac[0] is not None:
            pend_evac[0]()
        P.barrier()
        if stop_after == "C":
            return _finish(nc, P, dumps, {"AT": U2[:, 20480:28672]})

        mergedT = U1[:, 0:16384].rearrange("p (c n) -> p c n", c=8)
        Wo = U1[:, 16384:24576].rearrange("p (c n) -> p c n", c=8)
        wo_t = T()
        wo_ch = P.chan("wo")
        P.dma_multi("pool", wo_ch, [(Wo[:, 2 * k:2 * k + 2, :],
                                     w_o[k * 256:(k + 1) * 256, :].rearrange("(c p) n -> p c n", p=128))
                                    for k in range(4)], writes=[wo_t])
        x1 = [u1f32(49152, 4096).rearrange("p (i d) -> p i d", i=4),
              tmf32(8192, 4096).rearrange("p (i d) -> p i d", i=4)]
        x1_t = [[T() for _ in range(4)] for _ in range(2)]
        x1_ch = [P.chan("x1a"), P.chan("x1b")]

        def x_load(g):
            bf = g % 2
            P.dma("sp", x1_ch[bf], x1[bf][:, :, :],
                  x_ext[OWN0 + g * 512: OWN0 + (g + 1) * 512, :].rearrange("(i p) d -> p i d", p=128),
                  writes=x1_t[bf])

        x_load(0)
        w1s.ensure(0)
        sga = [tmf32(k * 2048, 512) for k in range(2)]
        sgb = [tmf32(4096 + k * 2048, 512) for k in range(2)]
        t1 = [tmf32(8192 + k * 2048, 512) for k in range(2)]
        t2 = [tmf32(12288 + k * 2048, 512) for k in range(2)]
        sga_t, sgb_t, t1_t, t2_t = [T(), T()], [T(), T()], [T(), T()], [T(), T()]
        it = 0
        for f in range(8):
            k = f % 2
            for tg in range(4):
                b0 = (it % 2) * 4
                s = it % 2
                it += 1
                tok = slice(tg * 512, (tg + 1) * 512)
                xtok = slice(OWN0 + tg * 512, OWN0 + (tg + 1) * 512)
                P.op("pe", lambda: [nc.tensor.matmul(bank(b0), lhsT=gAf[k][:, c, :], rhs=xnT[:, c, xtok],
                                                     start=(c == 0), stop=(c == 7)) for c in range(8)],
                     reads=[d1_t[k]], writes=[bankT[b0]])
                P.op("pe", lambda: [nc.tensor.matmul(bank(b0 + 1), lhsT=WAf[k][:, c, :], rhs=AT[:, c, tok],
                                                     start=(c == 0), stop=(c == 3)) for c in range(4)],
                     reads=[d1_t[k]], writes=[bankT[b0 + 1]])
                P.op("pe", lambda: [nc.tensor.matmul(bank(b0 + 2), lhsT=gBf[k][:, c, :], rhs=xnT[:, c, xtok],
                                                     start=(c == 0), stop=(c == 7)) for c in range(8)],
                     reads=[d1_t[k]], writes=[bankT[b0 + 2]])
                P.op("pe", lambda: [nc.tensor.matmul(bank(b0 + 3), lhsT=WBf[k][:, c, :], rhs=BT[:, c, tok],
                                                     start=(c == 0), stop=(c == 3)) for c in range(4)],
                     reads=[d1_t[k]], writes=[bankT[b0 + 3]])
                P.op("act", lambda: nc.scalar.activation(out=sga[s], in_=bank(b0), func=AF.Sigmoid),
                     reads=[bankT[b0]], writes=[sga_t[s]])
                P.op("act", lambda: nc.scalar.activation(out=sgb[s], in_=bank(b0 + 2), func=AF.Sigmoid),
                     reads=[bankT[b0 + 2]], writes=[sgb_t[s]])
                P.op("dve", lambda: nc.vector.tensor_tensor(out=t1[s], in0=bank(b0 + 1), in1=sga[s], op=ALU.mult),
                     reads=[bankT[b0 + 1], sga_t[s]], writes=[t1_t[s]])
                P.op("dve", lambda: nc.vector.tensor_tensor(out=t2[s], in0=bank(b0 + 3), in1=sgb[s], op=ALU.mult),
                     reads=[bankT[b0 + 3], sgb_t[s]], writes=[t2_t[s]])
                P.op("dve", lambda: nc.vector.tensor_tensor(out=mergedT[:, f, tok], in0=t1[s], in1=t2[s], op=ALU.add),
                     reads=[t1_t[s], t2_t[s]])
            if f + 2 < 8:
                load_d1(f + 2)
        P.barrier()
        if stop_after == "D1":
            return _finish(nc, P, dumps, {"mergedT": U1[:, 0:16384]})

        xst2 = U1[:, 32768:36864].rearrange("p (i d) -> p i d", i=4)
        ot = [u1f32(73728 + k * 2048, 512) for k in range(4)]
        xn2T = [U2[:, 0:4096].rearrange("p (c n) -> p c n", c=8),
                U2[:, 32768:36864].rearrange("p (c n) -> p c n", c=8)]
        hT = U2[:, 4096:20480].rearrange("p (c n) -> p c n", c=32)
        w2s = [U2[:, 20480 + k * 4096:20480 + (k + 1) * 4096].rearrange("p (c n) -> p c n", c=8)
               for k in range(3)]
        rl = [tmf32(k * 2048, 512) for k in range(2)]
        junkE = tm16(4096, 1024)
        xn2c_t = [[T() for _ in range(8)] for _ in range(2)]
        hTc_t = [T() for _ in range(32)]
        xs2_t = [T() for _ in range(4)]
        t_junkE = T()
        rl_t = [T(), T()]
        ot_t = [T() for _ in range(4)]
        ot_ch = [P.chan(f"ot{k}") for k in range(4)]
        st2_t = [T(), T()]
        w2st = Stream(P, "pool", "w2s", 3, w2s)
        for g in range(4):
            for c2 in range(2):
                for sbk in range(4):
                    w2st.add(w2[sbk * 1024:(sbk + 1) * 1024, c2 * 512:(c2 + 1) * 512]
                             .rearrange("(c p) n -> p c n", p=128))
        w2st.ensure(0)
        gbc = [0]
        occ = [0]

        def nextbank():
            b_ = gbc[0] % 4
            gbc[0] += 1
            return b_

        def wo_part1(g):
            bf = g % 2
            for i in range(4):
                for c2 in range(2):
                    b_ = nextbank()
                    P.op("pe", lambda: [nc.tensor.matmul(
                        bank(b_), lhsT=mergedT[:, k, g * 512 + i * 128: g * 512 + (i + 1) * 128],
                        rhs=Wo[:, k, c2 * 512:(c2 + 1) * 512], start=(k == 0), stop=(k == 7)) for k in range(8)],
                        reads=[wo_t], writes=[bankT[b_]])
                    P.op("dve", lambda: nc.vector.tensor_tensor(
                        out=x1[bf][:, i, c2 * 512:(c2 + 1) * 512], in0=bank(b_),
                        in1=x1[bf][:, i, c2 * 512:(c2 + 1) * 512], op=ALU.add),
                        reads=[bankT[b_]], writes=[x1_t[bf][i]])
                P.op("act", lambda: nc.scalar.activation(
                    out=junkE, in_=x1[bf][:, i, :], func=AF.Square, scale=1.0 / 32.0,
                    accum_out=stat[:, 4 * bf + i:4 * bf + i + 1]),
                    reads=[x1_t[bf][i]], writes=[st2_t[bf], t_junkE])
            P.op("act", lambda: nc.scalar.activation(out=stat[:, 20 + 4 * bf:24 + 4 * bf],
                                                     in_=stat[:, 4 * bf:4 * bf + 4], func=AF.Ln, bias=EPS),
                 reads=[st2_t[bf]], writes=[st2_t[bf]])
            P.op("act", lambda: nc.scalar.activation(out=stat[:, 40 + 4 * bf:44 + 4 * bf],
                                                     in_=stat[:, 20 + 4 * bf:24 + 4 * bf], func=AF.Exp, scale=-0.5),
                 reads=[st2_t[bf]], writes=[st2_t[bf]])
            for i in range(4):
                P.op("dve", lambda: nc.vector.tensor_scalar(
                    out=xst2[:, i, :], in0=x1[bf][:, i, :], scalar1=stat[:, 40 + 4 * bf + i:41 + 4 * bf + i],
                    scalar2=None, op0=ALU.mult),
                    reads=[x1_t[bf][i], st2_t[bf]], writes=[xs2_t[i]])

        def wo_part2(g):
            bf = g % 2
            for c in range(8):
                b_ = nextbank()
                P.op("pe", lambda: [nc.tensor.transpose(
                    bank16(b_)[:, i * 128:(i + 1) * 128], xst2[:, i, c * 128:(c + 1) * 128], ident)
                    for i in range(4)], reads=xs2_t, writes=[bankT[b_]])
                if c % 2 == 0:
                    P.op("dve", lambda: nc.vector.tensor_scalar(
                        out=xn2T[bf][:, c, :], in0=bank16(b_)[:, 0:512], scalar1=smp[:, 8 + c:9 + c], scalar2=None,
                        op0=ALU.mult), reads=[bankT[b_]], writes=[xn2c_t[bf][c]])
                else:
                    P.op("act", lambda: nc.scalar.activation(
                        out=xn2T[bf][:, c, :], in_=bank16(b_)[:, 0:512], func=AF.Copy, scale=smp[:, 8 + c:9 + c]),
                        reads=[bankT[b_]], writes=[xn2c_t[bf][c]])

        def w2_half(g, c2, mid=None):
            bf = g % 2
            for sbk in range(4):
                li = g * 8 + c2 * 4 + sbk
                w2st.ensure(li)
                sl_ = w2st.slot(li)
                for i in range(4):
                    P.op("pe", lambda: [nc.tensor.matmul(
                        bank(4 + i), lhsT=hT[:, sbk * 8 + kk, i * 128:(i + 1) * 128], rhs=w2s[sl_][:, kk, :],
                        start=(sbk == 0 and kk == 0), stop=(sbk == 3 and kk == 7)) for kk in range(8)],
                        reads=[w2st.tb[sl_]] + hTc_t[sbk * 8:sbk * 8 + 8],
                        writes=[bankT[4 + i]] if (sbk == 3 or sbk == 0) else [])
                if sbk == 1 and mid is not None:
                    mid()
            for i in range(4):
                o = occ[0] % 4
                occ[0] += 1
                P.op("dve", lambda: nc.vector.tensor_tensor(
                    out=ot[o], in0=bank(4 + i), in1=x1[bf][:, i, c2 * 512:(c2 + 1) * 512], op=ALU.add),
                    reads=[bankT[4 + i], x1_t[bf][i]], writes=[ot_t[o]])
                P.dma("sp", ot_ch[o], out[g * 512 + i * 128: g * 512 + (i + 1) * 128, c2 * 512:(c2 + 1) * 512],
                      ot[o], reads=[ot_t[o]])

        wo_part1(0)
        wo_part2(0)
        for g in range(4):
            bf = g % 2
            if g + 1 < 4:
                x_load(g + 1)
            if g >= 1:
                w2st.issue_upto(8 * g + 2)
            for fc in range(32):
                li = g * 8 + fc // 4
                w1s.ensure(li)
                sl_ = w1s.slot(li)
                b_ = nextbank()
                P.op("pe", lambda: [nc.tensor.matmul(
                    bank(b_), lhsT=ws3[sl_][:, k, (fc % 4) * 128:(fc % 4 + 1) * 128], rhs=xn2T[bf][:, k, :],
                    start=(k == 0), stop=(k == 7)) for k in range(8)],
                    reads=[w1s.tb[sl_]] + xn2c_t[bf], writes=[bankT[b_]])
                r = fc % 2
                P.op("act", lambda: nc.scalar.activation(out=rl[r], in_=bank(b_), func=AF.Relu),
                     reads=[bankT[b_]], writes=[rl_t[r]])
                P.op("dve", lambda: nc.vector.tensor_tensor(out=hT[:, fc, :], in0=rl[r], in1=rl[r], op=ALU.mult),
                     reads=[rl_t[r]], writes=[hTc_t[fc]])
            if g + 1 < 4:
                w1s.issue_upto(8 * (g + 1) + 2)
            w2_half(g, 0)
            if g + 1 < 4:
                wo_part1(g + 1)
                w2_half(g, 1, mid=lambda: wo_part2(g + 1))
            else:
                w2_half(g, 1)
        P.barrier()
        return _finish(nc, P, dumps, {})
    return nc


def _finish(nc, P, dumps, avail):
    if dumps:
        ch = P.chan("dbg")
        for name, ap in dumps.items():
            P.dma("sp", ch, ap[:, :], avail[name])
        P.barrier()
    return nc


def _ext_tile_rows(q, e):
    R0 = 32 * q
    l0 = 2 * e - 4
    if q == 0 and e == 0:
        return (6, 7)
    if q == 0 and e == 1:
        return (None, None)
    if q == 3 and e == 18:
        return (None, None)
    if q == 3 and e == 19:
        return (R0 + 24, R0 + 25)
    return (R0 + l0, R0 + l0 + 1)


def _bias_table(rpb, q, n):
    R0 = 32 * q
    kc = np.arange(64)[:, None]
    qc = np.arange(64)[None, :]
    wst = np.clip(qc - 8, 0, 48)
    colmask = (kc >= wst) & (kc < wst + 16)
    relc = np.clip(kc - qc + 15, 0, 30)
    tab = np.full((2, 64, 8, 5, 2, 64), NEG, np.float32)
    for j in range(5):
        rows = _ext_tile_rows(q, n + j)
        for krl in range(2):
            kr = rows[krl]
            if kr is None:
                continue
            for qrl in range(2):
                r = R0 + 2 * n + qrl
                rs = min(max(r - 4, 0), 120)
                if not (rs <= kr < rs + 8):
                    continue
                blk = rpb[:, kr - r + 7, :][:, relc]
                blk = np.where(colmask[None], blk, np.float32(NEG))
                tab[krl, :, :, j, qrl, :] = blk.transpose(1, 0, 2)
    return tab.reshape(128, 8 * 5 * 128)


def _host_inputs(x, norm1_g, w_in, q_norm_g, k_norm_g, rpb, conv_w, w_attn_branch,
                 w_conv_branch, w_o, norm2_g, w_mlp_in, w_mlp_out):
    f = lambda a: np.ascontiguousarray(np.asarray(a, dtype=np.float32))
    x = f(x)
    rpb0 = f(rpb)[0]
    smallp = np.zeros((128, 32), np.float32)
    smallp[:, 0:8] = f(norm1_g)[0].reshape(8, 128).T
    smallp[:, 8:16] = f(norm2_g)[0].reshape(8, 128).T
    smallp[:, 16] = np.tile(f(q_norm_g)[0], 2)
    smallp[:, 17] = np.tile(f(k_norm_g)[0], 2)
    smallp[:, 18:30] = f(conv_w)[0].reshape(3, 4, 128).transpose(2, 1, 0).reshape(128, 12)
    consts = np.zeros((128, 256), np.float32)
    consts[:, 0:128] = np.eye(128, dtype=np.float32)
    blk = np.arange(128) // 64
    consts[:, 128:256] = (blk[:, None] == blk[None, :]).astype(np.float32)
    shared = {
        "smallp": smallp, "consts": consts, "w_in": f(w_in)[0], "w_a": f(w_attn_branch)[0],
        "w_b": f(w_conv_branch)[0], "w_o": f(w_o)[0], "w1": f(w_mlp_in)[0], "w2": f(w_mlp_out)[0],
    }
    tabs = {}
    in_maps = []
    for core in range(N_CORES):
        b, q = core // 4, core % 4
        xg = x[b].reshape(128, 64, D)
        xe = np.zeros((40, 64, D), np.float32)
        for e in range(20):
            rows = _ext_tile_rows(q, e)
            for k in range(2):
                if rows[k] is not None and 0 <= rows[k] < 128:
                    xe[2 * e + k] = xg[rows[k]]
        if q not in tabs:
            tabs[q] = np.stack([_bias_table(rpb0, q, n) for n in (5, 0, 1, 14, 15)], axis=0)
        m = dict(shared)
        m["x_ext"] = xe.reshape(TEXT, D)
        m["btab"] = tabs[q]
        in_maps.append(m)
    return in_maps


_NC_CACHE = {}


def kernel(**inputs):
    in_maps = _host_inputs(**inputs)
    if "nc" not in _NC_CACHE:
        _NC_CACHE["nc"] = build_nc()
    res = run_bass_kernel_spmd(_NC_CACHE["nc"], in_maps, core_ids=list(range(N_CORES)))
    outs = [np.asarray(r["out"], dtype=np.float32).reshape(TOK, D) for r in res.results]
    full = np.concatenate(outs, axis=0).reshape(2, 8192, D)
    return full
```
